# Optimizing a Trainium2 kernel written in Bass

```python
import jax, jax.numpy as jnp
from jax import lax
import numpy as np

D_MODEL = 4096
BATCH = 1
SEQ = 8192
DEPTH = 2

D_CONF = D_MODEL // 2
D_SCONV = D_MODEL // 2
N_GROUPS = 16
CONF_KERNEL = 31
SCONV_KERNEL = 3
FFN_KERNEL = 3
D_FF = ((8 * D_MODEL // 3 + 255) // 256) * 256
SPLITS = (D_CONF, D_CONF, D_SCONV, D_SCONV, D_SCONV, D_MODEL, D_MODEL)
N_IN = sum(SPLITS)
N_ADA = 6 * D_MODEL
RMS_EPS = 1e-6
LN_EPS = 1e-5

kernel_name = "hybrid_conformer_shortconv_convffn_adaln"


def rms_norm(x, g):
    xf = x.astype(jnp.float32)
    y = xf * lax.rsqrt(jnp.mean(xf * xf, axis=-1, keepdims=True) + RMS_EPS)
    return (y * g.astype(jnp.float32)).astype(x.dtype)


def layer_norm(x, g, b):
    xf = x.astype(jnp.float32)
    mu = jnp.mean(xf, axis=-1, keepdims=True)
    var = jnp.mean(jnp.square(xf - mu), axis=-1, keepdims=True)
    y = (xf - mu) * lax.rsqrt(var + LN_EPS)
    return (y * g.astype(jnp.float32) + b.astype(jnp.float32)).astype(x.dtype)


def causal_dwconv(x, w):
    k = w.shape[0]
    c = x.shape[-1]
    return lax.conv_general_dilated(
        x, w[:, None, :].astype(x.dtype), window_strides=(1,), padding=[(k - 1, 0)],
        dimension_numbers=("NWC", "WIO", "NWC"), feature_group_count=c)


def modulate(h, shift, scale):
    return h * (1.0 + scale[:, None, :]) + shift[:, None, :]


def setup_inputs(seed: int = 0) -> dict:
    key = jax.random.key(seed)
    ks = jax.random.split(key, 24)
    f32 = jnp.float32
    L, D = DEPTH, D_MODEL

    def nrm(k, shape, scale):
        return jax.random.normal(k, shape, f32) * scale

    return {
        "x": nrm(ks[0], (BATCH, SEQ, D), 1.0),
        "c": nrm(ks[1], (BATCH, D), 1.0),
        "w_ada": nrm(ks[2], (L, D, N_ADA), 0.5 * D ** -0.5),
        "b_ada": nrm(ks[3], (L, N_ADA), 0.01),
        "g_mix": 1.0 + nrm(ks[4], (L, D), 0.02),
        "w_in": nrm(ks[5], (L, D, N_IN), D ** -0.5),
        "conf_w": nrm(ks[6], (L, CONF_KERNEL, D_CONF), CONF_KERNEL ** -0.5),
        "conf_b": nrm(ks[7], (L, D_CONF), 0.01),
        "ln_g": 1.0 + nrm(ks[8], (L, D_CONF), 0.02),
        "ln_b": nrm(ks[9], (L, D_CONF), 0.01),
        "w_a_out": nrm(ks[10], (L, D_CONF, D), D_CONF ** -0.5),
        "sconv_w": nrm(ks[11], (L, SCONV_KERNEL, D_SCONV), SCONV_KERNEL ** -0.5),
        "w_b_out": nrm(ks[12], (L, D_SCONV, D), D_SCONV ** -0.5),
        "w_o": nrm(ks[13], (L, D, D), D ** -0.5),
        "g_ffn": 1.0 + nrm(ks[14], (L, D), 0.02),
        "w_up": nrm(ks[15], (L, D, 2 * D_FF), D ** -0.5),
        "ffn_conv_w": nrm(ks[16], (L, FFN_KERNEL, 2 * D_FF), FFN_KERNEL ** -0.5),
        "w_down": nrm(ks[17], (L, D_FF, D), D_FF ** -0.5),
        "g_final": 1.0 + nrm(ks[18], (D,), 0.02),
    }


def reference(x, c, w_ada, b_ada, g_mix, w_in, conf_w, conf_b, ln_g, ln_b, w_a_out,
              sconv_w, w_b_out, w_o, g_ffn, w_up, ffn_conv_w, w_down, g_final):
    split_idx = [int(v) for v in np.cumsum(SPLITS)[:-1]]
    c_act = jax.nn.silu(c)
    for l in range(DEPTH):
        mod = c_act @ w_ada[l] + b_ada[l]
        sh1, sc1, gt1, sh2, sc2, gt2 = jnp.split(mod, 6, axis=-1)

        h = modulate(rms_norm(x, g_mix[l]), sh1, sc1)
        z = h @ w_in[l]
        a_v, a_g, s_b, s_c, s_x, pre_ga, pre_gb = jnp.split(z, split_idx, axis=-1)

        u = a_v * jax.nn.sigmoid(a_g)
        u = causal_dwconv(u, conf_w[l]) + conf_b[l]
        u = jax.nn.silu(layer_norm(u, ln_g[l], ln_b[l]))
        y_a = u @ w_a_out[l]

        v = causal_dwconv(s_c * s_x, sconv_w[l])
        y_b = (s_b * v) @ w_b_out[l]

        m = jax.nn.sigmoid(pre_ga) * y_a + jax.nn.sigmoid(pre_gb) * y_b
        x = x + gt1[:, None, :] * (m @ w_o[l])

        h = modulate(rms_norm(x, g_ffn[l]), sh2, sc2)
        up = causal_dwconv(h @ w_up[l], ffn_conv_w[l])
        f_gate, f_val = jnp.split(up, 2, axis=-1)
        x = x + gt2[:, None, :] * ((jax.nn.silu(f_gate) * f_val) @ w_down[l])

    return rms_norm(x, g_final)
```

```python
import numpy as np
from contextlib import ExitStack
import concourse.bass as bass
import concourse.mybir as mybir
from concourse.bass_utils import run_bass_kernel_spmd

F32 = mybir.dt.float32
BF16 = mybir.dt.bfloat16
ALU = mybir.AluOpType
AF = mybir.ActivationFunctionType

NCORES = 8
L = 2
D = 4096
SEQ = 8192
DC = 2048
DFF = 11008
NIN = 18432
TW = 576
HALO = 64
OWN = 512
NT = 2
NSL = 32
NGC = 86
QSZ = (22, 22, 21, 21)
QOFF = (0, 22, 44, 65)
RMS_EPS = 1e-6
LN_EPS = 1e-5
NSLOT = 4

CV_BADA = 0
CV_GMIX = 192
CV_CONFW = 224
CV_CONFB = 720
CV_LNG = 736
CV_LNB = 752
CV_SCW = 768
CV_GFFN = 816
CV_FCW = 848
CV_GFIN = 1364
NCV = 1396


class _Op:
    __slots__ = ("eng", "fn", "deps", "dma", "dma_val", "signal", "sig_val", "pos")


class Sched:
    ENGS = ("pe", "act", "dve", "pool", "sp")
    WINDOW = 6

    def __init__(self):
        self.ops = []
        self.cnt = {e: 0 for e in self.ENGS}
        self.last_w = {}
        self.readers = {}
        self.dma_cnt = {}

    def op(self, eng, fn, reads=(), writes=(), dma=None):
        o = _Op()
        o.eng = eng
        o.fn = fn
        o.dma = dma
        o.signal = False
        o.sig_val = 0
        o.pos = self.cnt[eng]
        self.cnt[eng] += 1
        deps = {}
        for r in reads:
            w = self.last_w.get(r)
            if w is not None:
                deps[w] = True
        for r in writes:
            w = self.last_w.get(r)
            if w is not None and w not in deps:
                deps[w] = False
            for rd in self.readers.get(r, ()):
                if rd not in deps:
                    deps[rd] = False
        for r in reads:
            self.readers.setdefault(r, []).append(o)
        for r in writes:
            self.last_w[r] = o
            self.readers[r] = []
        deps.pop(o, None)
        o.deps = deps
        if dma is not None:
            self.dma_cnt[dma] = self.dma_cnt.get(dma, 0) + 1
            o.dma_val = 16 * self.dma_cnt[dma]
        else:
            o.dma_val = 0
        self.ops.append(o)
        return o

    def _needs(self, d, o, raw):
        if d.dma is not None:
            return True
        if d.eng != o.eng:
            return True
        if o.dma is not None:
            return True
        if d.eng == "pe":
            return False
        return raw and (o.pos - d.pos) <= self.WINDOW

    def emit(self, nc, eng_sems, dma_sems):
        engh = {"pe": nc.tensor, "act": nc.scalar, "dve": nc.vector, "pool": nc.gpsimd, "sp": nc.sync}
        for o in self.ops:
            for d, raw in o.deps.items():
                if d.dma is None and self._needs(d, o, raw):
                    d.signal = True
        run = {e: 0 for e in self.ENGS}
        for o in self.ops:
            if o.dma is None and o.signal:
                run[o.eng] += 1
                o.sig_val = run[o.eng]
        seen = {e: {} for e in self.ENGS}
        for o in self.ops:
            h = engh[o.eng]
            waits = {}
            for d, raw in o.deps.items():
                if not self._needs(d, o, raw):
                    continue
                if d.dma is not None:
                    key, val = ("dma", d.dma), d.dma_val
                else:
                    key, val = ("eng", d.eng), d.sig_val
                if waits.get(key, 0) < val:
                    waits[key] = val
            sn = seen[o.eng]
            for key, val in waits.items():
                if sn.get(key, 0) >= val:
                    continue
                sn[key] = val
                sem = dma_sems[key[1]] if key[0] == "dma" else eng_sems[key[1]]
                h.wait_ge(sem, val)
            ins = o.fn(h)
            if o.dma is not None:
                ins.then_inc(dma_sems[o.dma], 16)
            elif o.signal:
                ins.then_inc(eng_sems[o.eng], 1)
        return {k: 16 * v for k, v in self.dma_cnt.items()}


def _halves(lo, ov=0):
    n = TW - lo
    h0 = lo + ((n // 2 + 1) // 2) * 2
    return [(lo, h0), (h0 - ov, TW)]


def build_program():
    nc = bass.Bass("TRN2", target_bir_lowering=False)
    S = Sched()

    def din(name, shape):
        return nc.dram_tensor(name, list(shape), F32, kind="ExternalInput")

    xin = din("xin", [NT, 128, NSL, TW])
    cbc = din("cbc", [128, D])
    hmk = din("hmk", [128, NT])
    cvec = din("cvec", [L, 128, NCV])
    w_in = din("w_in", [L, 144, 128, 32, 128])
    w_a = din("w_a", [L, 32, 128, 16, 128])
    w_b = din("w_b", [L, 32, 128, 16, 128])
    w_o = din("w_o", [L, 32, 128, 32, 128])
    w_up = din("w_up", [L, 172, 128, 32, 128])
    w_dn = din("w_dn", [L, 4, 32, 128, 22, 128])
    adaT = din("adaT", [L, 192, 128, D])
    outT = nc.dram_tensor("outT", [NT, 128, NSL, OWN], F32, kind="ExternalOutput")
    xsv = nc.dram_tensor("xsv", [NT, 128, NSL, TW], F32, kind="Internal")

    es = ExitStack()
    with es:
        es.enter_context(nc.allow_low_precision("bf16 matmul operands by design"))
        bigf = [es.enter_context(nc.sbuf_tensor(f"big{r}", [128, 16 * TW], F32)) for r in range(4)]
        bigb = [t.bitcast(BF16) for t in bigf]
        wsl = [es.enter_context(nc.sbuf_tensor(f"wsl{i}", [128, 32, 128], BF16)) for i in range(NSLOT)]
        cv = es.enter_context(nc.sbuf_tensor("cv", [128, L * NCV], F32))
        cact = es.enter_context(nc.sbuf_tensor("cact", [128, D], BF16))
        modr = es.enter_context(nc.sbuf_tensor("modr", [128, L * 192], F32))
        modc = es.enter_context(nc.sbuf_tensor("modc", [128, L * 64], F32))
        hm = es.enter_context(nc.sbuf_tensor("hm", [128, NT], F32))
        ones = es.enter_context(nc.sbuf_tensor("ones", [128, 128], F32))
        rsb = es.enter_context(nc.sbuf_tensor("rsb", [128, 3 * TW], F32))
        dummy = es.enter_context(nc.sbuf_tensor("dmy", [128, 8], F32))
        psb = [es.enter_context(nc.psum_tensor(f"ps{i}", [128, 512], F32)) for i in range(8)]
        eng_sems = {e: es.enter_context(nc.semaphore(f"s_{e}")) for e in Sched.ENGS}
        dma_keys = ([f"w{i}" for i in range(NSLOT)] + ["xl0", "xl1", "ev0", "ev1", "xt0", "xt1",
                                                        "out0", "out1", "cst0", "cst1", "cst2", "cst3"])
        dma_sems = {k: es.enter_context(nc.semaphore(f"d_{k}")) for k in dma_keys}

        def RF(r, i):
            return [("b", r, 2 * i), ("b", r, 2 * i + 1)]

        def RB(r, u):
            return [("b", r, u)]

        def PB(p):
            return [("ps", 2 * p), ("ps", 2 * p + 1)]

        def fsl(r, i, c0=0, c1=TW):
            return bigf[r][:, i * TW + c0: i * TW + c1]

        def bsl(r, u, c0=0, c1=TW):
            return bigb[r][:, u * TW + c0: u * TW + c1]

        def cvc(l, col, n=1):
            return cv[:, l * NCV + col: l * NCV + col + n]

        def A(out, in_, func, reads, writes, bias=None, scale=None):
            kw = {}
            if bias is not None:
                kw["bias"] = bias
            if scale is not None:
                kw["scale"] = scale
            S.op("act", lambda e: e.activation(out=out, in_=in_, func=func, **kw), reads=reads, writes=writes)

        def TT(out, in0, in1, op, reads, writes):
            S.op("dve", lambda e: e.tensor_tensor(out=out, in0=in0, in1=in1, op=op), reads=reads, writes=writes)

        def TS(out, in0, s1, s2, op0, op1, reads, writes):
            if op1 is None:
                S.op("dve", lambda e: e.tensor_scalar(out=out, in0=in0, scalar1=s1, scalar2=None, op0=op0),
                     reads=reads, writes=writes)
            else:
                S.op("dve", lambda e: e.tensor_scalar(out=out, in0=in0, scalar1=s1, scalar2=s2, op0=op0, op1=op1),
                     reads=reads, writes=writes)

        def STT(out, in0, scalar, in1, op0, op1, reads, writes, accum_out=None):
            if accum_out is None:
                S.op("dve", lambda e: e.scalar_tensor_tensor(out=out, in0=in0, scalar=scalar, in1=in1, op0=op0,
                                                             op1=op1), reads=reads, writes=writes)
            else:
                S.op("dve", lambda e: e.scalar_tensor_tensor(out=out, in0=in0, scalar=scalar, in1=in1, op0=op0,
                                                             op1=op1, accum_out=accum_out),
                     reads=reads, writes=writes)

        def RECIP(out, in_, reads, writes):
            S.op("dve", lambda e: e.reciprocal(out=out, in_=in_), reads=reads, writes=writes)

        def MEMSET(out, val, reads, writes):
            S.op("dve", lambda e: e.memset(out, val), reads=reads, writes=writes)

        def DMA(eng, out, in_, reads, writes, key):
            S.op(eng, lambda e: e.dma_start(out=out, in_=in_), reads=reads, writes=writes, dma=key)

        wstate = {"n": 0, "eq": 0.0, "credit": 0.0}
        bg = []

        def wtile(src_ap, kc):
            s = wstate["n"] % NSLOT
            wstate["n"] += 1
            DMA("pool", wsl[s][:, 0:kc, :], src_ap, [], [("w", s)], f"w{s}")
            wstate["eq"] += kc / 32.0
            rate = 0.25 if wstate["eq"] < 80 else (0.72 if wstate["eq"] < 245 else 0.30)
            wstate["credit"] += rate * kc / 32.0
            while wstate["credit"] >= 1.0 and bg:
                wstate["credit"] -= 1.0
                bg.pop(0)()
            return s

        def ada_tile(l, j):
            s = wstate["n"] % NSLOT
            wstate["n"] += 1
            dst = wsl[s][:, :, :].rearrange("p a b -> p (a b)")
            DMA("pool", dst, adaT[l, j], [], [("w", s)], f"w{s}")
            acc = modr[:, l * 192 + j: l * 192 + j + 1]
            STT(dst, dst, 1.0, cact[:, :], ALU.mult, ALU.mult, [("w", s), ("cact",)], [("w", s), ("modr", l, j)],
                accum_out=acc)

        pstate = {"n": 0, "set": (0, 1, 2, 3)}

        def ppair():
            st = pstate["set"]
            p = st[pstate["n"] % len(st)]
            pstate["n"] += 1
            return p

        def mmgroup(slot, kc_n, rhs_fn, hv, p, reads, kc_reads=None):
            plan = []
            for kc in range(kc_n):
                for i, (c0, c1) in enumerate(hv):
                    plan.append((psb[2 * p + i][:, 0:c1 - c0], wsl[slot][:, kc, :], rhs_fn(kc, c0, c1),
                                 kc == 0, kc == kc_n - 1))

            def mk(sub):
                def fn(pe):
                    last = None
                    for (o, lh, rh, st, sp_) in sub:
                        last = pe.matmul(o, lhsT=lh, rhs=rh, start=st, stop=sp_)
                    return last
                return fn
            if kc_reads is None:
                S.op("pe", mk(plan), reads=[("w", slot)] + list(reads), writes=PB(p))
            else:
                nh = len(hv)
                for kc in range(kc_n):
                    S.op("pe", mk(plan[kc * nh:(kc + 1) * nh]), reads=[("w", slot)] + list(kc_reads(kc)),
                         writes=PB(p))

        def statmm(src_fn, hv, p, first, last, reads):
            plan = [(psb[2 * p + i][:, 0:c1 - c0], src_fn(c0, c1)) for i, (c0, c1) in enumerate(hv)]

            def fn(pe):
                ins = None
                for (o, rh) in plan:
                    ins = pe.matmul(o, lhsT=ones[:, :], rhs=rh, start=first, stop=last)
                return ins
            S.op("pe", fn, reads=list(reads), writes=PB(p))

        DMA("sp", cv[:, :].rearrange("p (l n) -> p l n", l=L), cvec.ap().rearrange("l p n -> p l n"), [],
            [("cv",)], "cst0")
        DMA("sp", hm[:, :], hmk.ap(), [], [("hm",)], "cst1")
        ctmp = bigf[3][:, 0:D]
        cres = [("b", 3, u) for u in range(16)]
        DMA("sp", ctmp, cbc.ap(), [], cres, "cst2")
        A(cact[:, :], ctmp, AF.Silu, cres, [("cact",)])
        MEMSET(ones[:, :], 1.0, [], [("ones",)])

        def mod_finish(l, js, tag):
            j0, j1 = js
            mres = [("modr", l, j) for j in range(j0, j1)]
            MEMSET(dummy[:, :], 0.0, mres, [("dummy",)])
            sl = modr[:, l * 192 + j0: l * 192 + j1]
            TT(sl, sl, cvc(l, CV_BADA + j0, j1 - j0), ALU.add, [("dummy",), ("cv",)] + mres, [("modf", l, tag)])

        def mod_derive(l, which):
            sc0 = 32 if which == 1 else 128
            gcol = CV_GMIX if which == 1 else CV_GFFN
            dst = modc[:, l * 64 + (which - 1) * 32: l * 64 + which * 32]
            STT(dst, modr[:, l * 192 + sc0: l * 192 + sc0 + 32], 1.0, cvc(l, gcol, 32), ALU.add, ALU.mult,
                [("modf", l, f"s{which}"), ("cv",)], [("modA", l, which)])

        def modcol(l, grp, s):
            return modr[:, l * 192 + grp * 32 + s: l * 192 + grp * 32 + s + 1]

        def Acol(l, which, s):
            return modc[:, l * 64 + (which - 1) * 32 + s: l * 64 + (which - 1) * 32 + s + 1]

        def bg_step(n=1):
            pass

        for j in range(0, 64):
            ada_tile(0, j)
        mod_finish(0, (0, 64), "s1")
        mod_derive(0, 1)

        def _mk(l, j):
            return lambda: ada_tile(l, j)

        def _fin(l, js, tag, which):
            def f():
                mod_finish(l, js, tag)
                if which:
                    mod_derive(l, which)
            return f
        for j in range(64, 96):
            bg.append(_mk(0, j))
        bg.append(_fin(0, (64, 96), "g1", 0))
        for j in range(96, 160):
            bg.append(_mk(0, j))
        bg.append(_fin(0, (96, 160), "s2", 2))
        for j in range(160, 192):
            bg.append(_mk(0, j))
        bg.append(_fin(0, (160, 192), "g2", 0))
        for ll in range(1, L):
            for j in range(0, 64):
                bg.append(_mk(ll, j))
            bg.append(_fin(ll, (0, 64), "s1", 1))
            for j in range(64, 96):
                bg.append(_mk(ll, j))
            bg.append(_fin(ll, (64, 96), "g1", 0))
            for j in range(96, 160):
                bg.append(_mk(ll, j))
            bg.append(_fin(ll, (96, 160), "s2", 2))
            for j in range(160, 192):
                bg.append(_mk(ll, j))
            bg.append(_fin(ll, (160, 192), "g2", 0))

        def bg_flush_until(l, tag):
            while bg and ("modf", l, tag) not in S.last_w:
                bg.pop(0)()

        def x_loc(xr, s):
            return xr[s // 16], s % 16

        def rs(slot, c0, c1):
            return rsb[:, slot * TW + c0: slot * TW + c1]

        def rstd_from_stats(p, hv, ndim, eps, slot):
            for i, (c0, c1) in enumerate(hv):
                A(rs(slot, c0, c1), psb[2 * p + i][:, 0:c1 - c0], AF.Sqrt, [("ps", 2 * p + i)], [("rsb", slot, i)],
                  bias=eps, scale=1.0 / ndim)
                RECIP(rs(slot, c0, c1), rs(slot, c0, c1), [("rsb", slot, i)], [("rsb", slot, i)])

        def make_h(l, which, xr, hreg, treg, lo, slot):
            bgrp = 0 if which == 1 else 3
            for s in range(NSL):
                r, i = x_loc(xr, s)
                tt = s % 2
                STT(fsl(treg, tt, lo), fsl(r, i, lo), Acol(l, which, s), rs(slot, lo, TW), ALU.mult, ALU.mult,
                    RF(r, i) + [("rsb", slot, 0), ("rsb", slot, 1), ("modA", l, which)], RF(treg, tt))
                A(bsl(hreg, s, lo), fsl(treg, tt, lo), AF.Identity, RF(treg, tt) + [("modf", l, f"s{which}")],
                  RB(hreg, s), bias=modcol(l, bgrp, s), scale=1.0)

        def x_stats_acc(r, i, treg, sqslab, accslab, first, lo):
            if first:
                A(fsl(treg, accslab, lo), fsl(r, i, lo), AF.Square, RF(r, i), RF(treg, accslab))
            else:
                A(fsl(treg, sqslab, lo), fsl(r, i, lo), AF.Square, RF(r, i), RF(treg, sqslab))
                TT(fsl(treg, accslab, lo), fsl(treg, accslab, lo), fsl(treg, sqslab, lo), ALU.add,
                   RF(treg, accslab) + RF(treg, sqslab), RF(treg, accslab))

        def x_stats_fin(treg, accslab, hv):
            p = ppair()
            statmm(lambda c0, c1: fsl(treg, accslab, c0, c1), hv, p, True, True, RF(treg, accslab) + [("ones",)])
            return p

        def quarter(t, l, xr, free, first, nxt):
            hmt = hm[:, t:t + 1]
            src = xin if l == 0 else xsv
            pstate["set"] = (0, 1, 2, 3)
            if l == 0:
                hv0 = _halves(0)
                for s in range(NSL):
                    r, i = x_loc(xr, s)
                    x_stats_acc(r, i, free[1], 12 + (s % 2), 14, s == 0, 0)
                rstd_from_stats(x_stats_fin(free[1], 14, hv0), hv0, float(D), RMS_EPS, t)

            if True:
                zl = 0 if l == 0 else 32
                cl = zl + 30
                gl = cl + 2
                ra, rb = xr
                rc, rd = free
                hreg, treg = rc, rd
                bg_flush_until(l, "s1")
                make_h(l, 1, xr, hreg, treg, zl, t)
                pstate["set"] = (0, 1, 2, 3)
                hvz = _halves(zl)
                hvc = _halves(cl)
                hreads = [("b", hreg, u) for u in range(32)]

                def hrhs(kc, c0, c1, hreg=hreg):
                    return bsl(hreg, kc, c0, c1)

                def zgroup(tile_idx, hv, first=False):
                    slot = wtile(w_in[l, tile_idx], 32)
                    p = ppair()
                    mmgroup(slot, 32, hrhs, hv, p, hreads,
                            kc_reads=((lambda kc, hreg=hreg: [("b", hreg, kc)]) if first else None))
                    return p

                def ln_acc(i):
                    if i == 0:
                        A(fsl(treg, 0, cl), fsl(ra, i, cl), AF.Identity, RF(ra, i), RF(treg, 0))
                        A(fsl(treg, 1, cl), fsl(ra, i, cl), AF.Square, RF(ra, i), RF(treg, 1))
                    else:
                        sq = 12 + (i % 2)
                        TT(fsl(treg, 0, cl), fsl(treg, 0, cl), fsl(ra, i, cl), ALU.add, RF(treg, 0) + RF(ra, i),
                           RF(treg, 0))
                        A(fsl(treg, sq, cl), fsl(ra, i, cl), AF.Square, RF(ra, i), RF(treg, sq))
                        TT(fsl(treg, 1, cl), fsl(treg, 1, cl), fsl(treg, sq, cl), ALU.add, RF(treg, 1) + RF(treg, sq),
                           RF(treg, 1))

                for i in range(16):
                    pv = zgroup(i, hvz, first=(i == 0))
                    pg = zgroup(16 + i, hvz)
                    sg = 2 + (i % 2)
                    ut = 4 + (i % 2)
                    for k, (c0, c1) in enumerate(hvz):
                        A(fsl(treg, sg, c0, c1), psb[2 * pg + k][:, 0:c1 - c0], AF.Sigmoid, [("ps", 2 * pg + k)],
                          RF(treg, sg))
                        TT(fsl(treg, ut, c0, c1), psb[2 * pv + k][:, 0:c1 - c0], fsl(treg, sg, c0, c1), ALU.mult,
                           [("ps", 2 * pv + k)] + RF(treg, sg), RF(treg, ut))
                    if zl < HALO:
                        TS(fsl(treg, ut, zl, HALO), fsl(treg, ut, zl, HALO), hmt, None, ALU.mult, None,
                           RF(treg, ut) + [("hm",)], RF(treg, ut))
                    if i > 0:
                        ln_acc(i - 1)
                    TS(fsl(ra, i, cl), fsl(treg, ut, cl, TW), cvc(l, CV_CONFW + 30 * 16 + i), cvc(l, CV_CONFB + i),
                       ALU.mult, ALU.add, RF(treg, ut) + [("cv",)], RF(ra, i))
                    for k in range(30):
                        sh = 30 - k
                        STT(fsl(ra, i, cl), fsl(treg, ut, cl - sh, TW - sh), cvc(l, CV_CONFW + k * 16 + i),
                            fsl(ra, i, cl), ALU.mult, ALU.add, RF(treg, ut) + RF(ra, i) + [("cv",)], RF(ra, i))
                    pc = zgroup(48 + i, hvz)
                    px = zgroup(64 + i, hvz)
                    pb = zgroup(32 + i, hvz)
                    sc_t = 6 + (i % 2)
                    cx_t = 8 + (i % 2)
                    v_t = 10 + (i % 2)
                    for k, (c0, c1) in enumerate(hvz):
                        A(fsl(treg, sc_t, c0, c1), psb[2 * pc + k][:, 0:c1 - c0], AF.Identity, [("ps", 2 * pc + k)],
                          RF(treg, sc_t))
                        TT(fsl(treg, cx_t, c0, c1), psb[2 * px + k][:, 0:c1 - c0], fsl(treg, sc_t, c0, c1), ALU.mult,
                           [("ps", 2 * px + k)] + RF(treg, sc_t), RF(treg, cx_t))
                    if zl < HALO:
                        TS(fsl(treg, cx_t, zl, HALO), fsl(treg, cx_t, zl, HALO), hmt, None, ALU.mult, None,
                           RF(treg, cx_t) + [("hm",)], RF(treg, cx_t))
                    TS(fsl(treg, v_t, cl), fsl(treg, cx_t, cl, TW), cvc(l, CV_SCW + 2 * 16 + i), None, ALU.mult, None,
                       RF(treg, cx_t) + [("cv",)], RF(treg, v_t))
                    for k in range(2):
                        sh = 2 - k
                        STT(fsl(treg, v_t, cl), fsl(treg, cx_t, cl - sh, TW - sh), cvc(l, CV_SCW + k * 16 + i),
                            fsl(treg, v_t, cl), ALU.mult, ALU.add, RF(treg, cx_t) + RF(treg, v_t) + [("cv",)],
                            RF(treg, v_t))
                    for k, (c0, c1) in enumerate(hvz):
                        a0 = max(c0, cl)
                        TT(bsl(rb, 16 + i, a0, c1), psb[2 * pb + k][:, a0 - c0:c1 - c0], fsl(treg, v_t, a0, c1),
                           ALU.mult, [("ps", 2 * pb + k)] + RF(treg, v_t), RB(rb, 16 + i))

                ureads = [("b", rb, u) for u in range(16)]
                sreads = [("b", rb, u) for u in range(16, 32)]

                def urhs(kc, c0, c1, rb=rb):
                    return bsl(rb, kc, c0, c1)

                def srhs(kc, c0, c1, rb=rb):
                    return bsl(rb, 16 + kc, c0, c1)

                def gates(f):
                    sga = 2 + (f % 2)
                    sgb = 4 + (f % 2)
                    pga = zgroup(80 + f, hvc)
                    for k, (c0, c1) in enumerate(hvc):
                        A(fsl(treg, sga, c0, c1), psb[2 * pga + k][:, 0:c1 - c0], AF.Sigmoid, [("ps", 2 * pga + k)],
                          RF(treg, sga))
                    pgb = zgroup(112 + f, hvc)
                    for k, (c0, c1) in enumerate(hvc):
                        A(fsl(treg, sgb, c0, c1), psb[2 * pgb + k][:, 0:c1 - c0], AF.Sigmoid, [("ps", 2 * pgb + k)],
                          RF(treg, sgb))

                def ys(f):
                    sga = 2 + (f % 2)
                    sgb = 4 + (f % 2)
                    slot = wtile(w_a[l, f], 16)
                    pya = ppair()
                    mmgroup(slot, 16, urhs, hvc, pya, ureads)
                    for k, (c0, c1) in enumerate(hvc):
                        TT(fsl(treg, sga, c0, c1), psb[2 * pya + k][:, 0:c1 - c0], fsl(treg, sga, c0, c1), ALU.mult,
                           [("ps", 2 * pya + k)] + RF(treg, sga), RF(treg, sga))
                    slot = wtile(w_b[l, f], 16)
                    pyb = ppair()
                    mmgroup(slot, 16, srhs, hvc, pyb, sreads)
                    for k, (c0, c1) in enumerate(hvc):
                        TT(fsl(treg, sgb, c0, c1), psb[2 * pyb + k][:, 0:c1 - c0], fsl(treg, sgb, c0, c1), ALU.mult,
                           [("ps", 2 * pyb + k)] + RF(treg, sgb), RF(treg, sgb))
                    TT(bsl(ra, f, cl), fsl(treg, sga, cl), fsl(treg, sgb, cl), ALU.add,
                       RF(treg, sga) + RF(treg, sgb), RB(ra, f))

                gates(0)
                ln_acc(15)
                p1 = ppair()
                statmm(lambda c0, c1: fsl(treg, 0, c0, c1), hvc, p1, True, True, RF(treg, 0) + [("ones",)])
                p2 = ppair()
                statmm(lambda c0, c1: fsl(treg, 1, c0, c1), hvc, p2, True, True, RF(treg, 1) + [("ones",)])
                MU, RI, T2 = 14, 15, 13
                for k, (c0, c1) in enumerate(hvc):
                    A(fsl(treg, MU, c0, c1), psb[2 * p1 + k][:, 0:c1 - c0], AF.Identity, [("ps", 2 * p1 + k)],
                      RF(treg, MU), scale=1.0 / DC)
                TT(fsl(treg, T2, cl), fsl(treg, MU, cl), fsl(treg, MU, cl), ALU.mult, RF(treg, MU), RF(treg, T2))
                for k, (c0, c1) in enumerate(hvc):
                    STT(fsl(treg, RI, c0, c1), psb[2 * p2 + k][:, 0:c1 - c0], 1.0 / DC, fsl(treg, T2, c0, c1),
                        ALU.mult, ALU.subtract, [("ps", 2 * p2 + k)] + RF(treg, T2), RF(treg, RI))
                A(fsl(treg, RI, cl), fsl(treg, RI, cl), AF.Sqrt, RF(treg, RI), RF(treg, RI), bias=LN_EPS, scale=1.0)
                RECIP(fsl(treg, RI, cl), fsl(treg, RI, cl), RF(treg, RI), RF(treg, RI))
                for i in range(16):
                    TT(fsl(ra, i, cl), fsl(ra, i, cl), fsl(treg, MU, cl), ALU.subtract, RF(ra, i) + RF(treg, MU),
                       RF(ra, i))
                    TT(fsl(ra, i, cl), fsl(ra, i, cl), fsl(treg, RI, cl), ALU.mult, RF(ra, i) + RF(treg, RI),
                       RF(ra, i))
                    A(bsl(rb, i, cl), fsl(ra, i, cl), AF.Silu, RF(ra, i) + [("cv",)], RB(rb, i),
                      bias=cvc(l, CV_LNB + i), scale=cvc(l, CV_LNG + i))
                gates(1)
                for f in range(32):
                    ys(f)
                    if f + 2 < 32:
                        gates(f + 2)

                bg_flush_until(l, "g1")
                nxr = (rc, rb)
                mreads = [("b", ra, u) for u in range(32)]

                def mrhs(kc, c0, c1, ra=ra):
                    return bsl(ra, kc, c0, c1)

                for f in range(32):
                    slot = wtile(w_o[l, f], 32)
                    p = ppair()
                    mmgroup(slot, 32, mrhs, hvc, p, mreads)
                    bg_step()
                    xt = 6 + (f % 2)
                    DMA("sp", fsl(treg, xt), src[t, :, f, :], ([("xsv", t, f // 16)] if l > 0 else []),
                        RF(treg, xt), f"xt{f % 2}")
                    nr, ni = x_loc(nxr, f)
                    for k, (c0, c1) in enumerate(hvc):
                        STT(fsl(nr, ni, c0, c1), psb[2 * p + k][:, 0:c1 - c0], modcol(l, 2, f), fsl(treg, xt, c0, c1),
                            ALU.mult, ALU.add, [("ps", 2 * p + k), ("modf", l, "g1")] + RF(treg, xt), RF(nr, ni))
                    x_stats_acc(nr, ni, treg, 12 + (f % 2), 14, f == 0, cl)
                rstd_from_stats(x_stats_fin(treg, 14, hvc), hvc, float(D), RMS_EPS, 2)
                xr = nxr
                free = [ra, rd]

                hreg, treg = ra, rd
                bg_flush_until(l, "s2")
                make_h(l, 2, xr, hreg, treg, cl, 2)
                hreads = [("b", hreg, u) for u in range(32)]

                def hrhs2(kc, c0, c1, hreg=hreg):
                    return bsl(hreg, kc, c0, c1)

                hvu = _halves(cl, ov=2)
                hvg = _halves(gl)
                (a0_, a1_), (b0_, b1_) = hvu
                for q in range(4):
                    for cc in range(QSZ[q]):
                        c = QOFF[q] + cc
                        outs = []
                        for half, fcol in ((0, c), (1, NGC + c)):
                            slot = wtile(w_up[l, fcol], 32)
                            p = ppair()
                            mmgroup(slot, 32, hrhs2, hvu, p, hreads,
                                    kc_reads=((lambda kc, hreg=hreg: [("b", hreg, kc)])
                                              if (q == 0 and cc == 0 and half == 0) else None))
                            bg_step()
                            ct = (11 if half == 0 else 13) + (cc % 2)
                            outs.append(ct)
                            if cl < HALO:
                                TS(psb[2 * p][:, 0:HALO - cl], psb[2 * p][:, 0:HALO - cl], hmt, None, ALU.mult, None,
                                   [("ps", 2 * p), ("hm",)], [("ps", 2 * p)])
                            for k, (o0, o1, base) in enumerate(((gl, a1_, a0_), (a1_, TW, b0_))):
                                A(fsl(treg, ct, o0, o1), psb[2 * p + k][:, o0 - base:o1 - base], AF.Identity,
                                  [("ps", 2 * p + k), ("cv",)], RF(treg, ct), scale=cvc(l, CV_FCW + 2 * 172 + fcol))
                                for tap in range(2):
                                    sh = 2 - tap
                                    STT(fsl(treg, ct, o0, o1), psb[2 * p + k][:, o0 - sh - base:o1 - sh - base],
                                        cvc(l, CV_FCW + tap * 172 + fcol), fsl(treg, ct, o0, o1), ALU.mult, ALU.add,
                                        [("ps", 2 * p + k), ("cv",)] + RF(treg, ct), RF(treg, ct))
                        gt_, vt_ = outs
                        A(fsl(treg, gt_, gl), fsl(treg, gt_, gl), AF.Silu, RF(treg, gt_), RF(treg, gt_))
                        TT(bsl(treg, cc, gl), fsl(treg, gt_, gl), fsl(treg, vt_, gl), ALU.mult,
                           RF(treg, gt_) + RF(treg, vt_), RB(treg, cc))
                    if q == 3 and nxt is not None:
                        nt_, nl_ = nxt
                        nsrc = xin if nl_ == 0 else xsv
                        DMA("sp", bigf[hreg][:, :].rearrange("p (s j) -> p s j", s=16), nsrc[nt_, :, 0:16, :],
                            ([("xsv", nt_, 0)] if nl_ > 0 else []), [("b", hreg, u) for u in range(32)], "xl0")
                    bg_flush_until(l, "g2")
                    greads = [("b", treg, u) for u in range(QSZ[q])]

                    def grhs(kc, c0, c1, treg=treg):
                        return bsl(treg, kc, c0, c1)

                    for f in range(32):
                        slot = wtile(w_dn[l, q, f][:, 0:QSZ[q], :], QSZ[q])
                        p = ppair()
                        mmgroup(slot, QSZ[q], grhs, hvg, p, greads)
                        bg_step()
                        r, i = x_loc(xr, f)
                        for k, (c0, c1) in enumerate(hvg):
                            STT(fsl(r, i, c0, c1), psb[2 * p + k][:, 0:c1 - c0], modcol(l, 5, f), fsl(r, i, c0, c1),
                                ALU.mult, ALU.add, [("ps", 2 * p + k), ("modf", l, "g2")] + RF(r, i), RF(r, i))
                        if q == 3:
                            x_stats_acc(r, i, treg, 11 + (f % 2), 13, f == 0, gl)
                    if q == 3:
                        rstd_from_stats(x_stats_fin(treg, 13, hvg), hvg, float(D), RMS_EPS, t)
                free = [ra, rd]

            ra, rd = free
            if l == 0:
                for k, rr in enumerate(xr):
                    DMA("sp", xsv[t, :, 16 * k:16 * (k + 1), :], bigf[rr][:, :].rearrange("p (s j) -> p s j", s=16),
                        [("b", rr, u) for u in range(32)], [("xsv", t, k)], f"ev{k}")
            else:
                for s in range(NSL):
                    r, i = x_loc(xr, s)
                    ot = 14 + (s % 2)
                    STT(fsl(rd, ot, HALO), fsl(r, i, HALO), cvc(0, CV_GFIN + s), rs(t, HALO, TW), ALU.mult, ALU.mult,
                        RF(r, i) + [("rsb", t, 0), ("rsb", t, 1), ("cv",)], RF(rd, ot))
                    DMA("sp", outT[t, :, s, :], fsl(rd, ot, HALO), RF(rd, ot), [("out", t, s)], f"out{s % 2}")
            return xr, free

        order = [(0, 0), (1, 0), (0, 1), (1, 1)]
        xr, free = (0, 1), [2, 3]
        for k in range(2):
            DMA("sp", bigf[xr[k]][:, :].rearrange("p (s j) -> p s j", s=16), xin[0, :, 16 * k:16 * (k + 1), :],
                [], [("b", xr[k], u) for u in range(32)], f"xl{k}")
        for qi, (t, l) in enumerate(order):
            nxt = order[qi + 1] if qi + 1 < len(order) else None
            xr, free = quarter(t, l, xr, free, qi == 0, nxt)
            if nxt is not None:
                nt_, nl_ = nxt
                nsrc = xin if nl_ == 0 else xsv
                DMA("sp", bigf[xr[0]][:, :].rearrange("p (s j) -> p s j", s=16), nsrc[nt_, :, 16:32, :],
                    ([("xsv", nt_, 1)] if nl_ > 0 else []), [("b", xr[0], u) for u in range(32)], "xl1")
                xr, free = (free[0], xr[0]), [xr[1], free[1]]
        while bg:
            bg.pop(0)()

        finals = S.emit(nc, eng_sems, dma_sems)
        for k in ("out0", "out1"):
            nc.sync.wait_ge(dma_sems[k], finals[k])
    return nc


def _colmajor(v, n):
    return np.ascontiguousarray(v.reshape(n, 128).T)


def _tile_w(w):
    K, N = w.shape
    return np.ascontiguousarray(w.reshape(K // 128, 128, N // 128, 128).transpose(2, 1, 0, 3))


def _prep_shared(inp):
    f32 = np.float32
    sh = {}
    sh["w_in"] = np.stack([_tile_w(np.asarray(inp["w_in"][l], f32)) for l in range(L)])
    sh["w_a"] = np.stack([_tile_w(np.asarray(inp["w_a_out"][l], f32)) for l in range(L)])
    sh["w_b"] = np.stack([_tile_w(np.asarray(inp["w_b_out"][l], f32)) for l in range(L)])
    sh["w_o"] = np.stack([_tile_w(np.asarray(inp["w_o"][l], f32)) for l in range(L)])
    sh["w_up"] = np.stack([_tile_w(np.asarray(inp["w_up"][l], f32)) for l in range(L)])
    wdn = np.zeros((L, 4, 32, 128, 22, 128), f32)
    for l in range(L):
        wd = _tile_w(np.asarray(inp["w_down"][l], f32))
        for q in range(4):
            wdn[l, q, :, :, 0:QSZ[q], :] = wd[:, :, QOFF[q]:QOFF[q] + QSZ[q], :]
    sh["w_dn"] = wdn
    sh["adaT"] = np.stack([np.ascontiguousarray(np.asarray(inp["w_ada"][l], f32).T).reshape(192, 128, D)
                           for l in range(L)])
    cvs = np.zeros((L, 128, NCV), f32)
    for l in range(L):
        cvs[l, :, CV_BADA:CV_BADA + 192] = _colmajor(np.asarray(inp["b_ada"][l], f32), 192)
        cvs[l, :, CV_GMIX:CV_GMIX + 32] = _colmajor(np.asarray(inp["g_mix"][l], f32), 32)
        cw = np.asarray(inp["conf_w"][l], f32)
        for k in range(31):
            cvs[l, :, CV_CONFW + k * 16:CV_CONFW + (k + 1) * 16] = _colmajor(cw[k], 16)
        cvs[l, :, CV_CONFB:CV_CONFB + 16] = _colmajor(np.asarray(inp["conf_b"][l], f32), 16)
        cvs[l, :, CV_LNG:CV_LNG + 16] = _colmajor(np.asarray(inp["ln_g"][l], f32), 16)
        cvs[l, :, CV_LNB:CV_LNB + 16] = _colmajor(np.asarray(inp["ln_b"][l], f32), 16)
        sw = np.asarray(inp["sconv_w"][l], f32)
        for k in range(3):
            cvs[l, :, CV_SCW + k * 16:CV_SCW + (k + 1) * 16] = _colmajor(sw[k], 16)
        cvs[l, :, CV_GFFN:CV_GFFN + 32] = _colmajor(np.asarray(inp["g_ffn"][l], f32), 32)
        fw = np.asarray(inp["ffn_conv_w"][l], f32)
        for k in range(3):
            cvs[l, :, CV_FCW + k * 172:CV_FCW + (k + 1) * 172] = _colmajor(fw[k], 172)
        cvs[l, :, CV_GFIN:CV_GFIN + 32] = _colmajor(np.asarray(inp["g_final"], f32), 32)
    sh["cvec"] = cvs
    sh["cbc"] = np.ascontiguousarray(np.broadcast_to(np.asarray(inp["c"], f32).reshape(1, D), (128, D)))
    return sh


_NC_CACHE = {}


def kernel(**inputs):
    f32 = np.float32
    x = np.asarray(inputs["x"], f32)[0]
    xpad = np.concatenate([np.zeros((HALO, D), f32), x], axis=0)
    shared = _prep_shared(inputs)
    in_maps = []
    for c in range(NCORES):
        xt = np.empty((NT, 128, NSL, TW), f32)
        for t in range(NT):
            r0 = c * (NT * OWN) + t * OWN
            blk = xpad[r0:r0 + TW]
            xt[t] = blk.T.reshape(NSL, 128, TW).transpose(1, 0, 2)
        hmk = np.ones((128, NT), f32)
        if c == 0:
            hmk[:, 0] = 0.0
        m = dict(shared)
        m["xin"] = xt
        m["hmk"] = hmk
        in_maps.append(m)
    if "nc" not in _NC_CACHE:
        _NC_CACHE["nc"] = build_program()
    nc = _NC_CACHE["nc"]
    res = run_bass_kernel_spmd(nc, in_maps, core_ids=list(range(NCORES)))
    out = np.empty((SEQ, D), f32)
    for c in range(NCORES):
        o = np.asarray(res.results[c]["outT"])
        for t in range(NT):
            r0 = c * (NT * OWN) + t * OWN
            out[r0:r0 + OWN] = o[t].transpose(2, 1, 0).reshape(OWN, D)
    return out[None]
```

```python
import numpy as np
from contextlib import ExitStack
import concourse.bass as bass
import concourse.mybir as mybir
from concourse.bass_utils import run_bass_kernel_spmd

F32 = mybir.dt.float32
BF16 = mybir.dt.bfloat16
ALU = mybir.AluOpType
AF = mybir.ActivationFunctionType

NCORES = 8
L = 2
D = 4096
SEQ = 8192
DC = 2048
DFF = 11008
NIN = 18432
TW = 576
HALO = 64
OWN = 512
NT = 2
NSL = 32
NGC = 86
QSZ = (22, 22, 21, 21)
QOFF = (0, 22, 44, 65)
RMS_EPS = 1e-6
LN_EPS = 1e-5
NSLOT = 4

CV_BADA = 0
CV_GMIX = 192
CV_CONFW = 224
CV_CONFB = 720
CV_LNG = 736
CV_LNB = 752
CV_SCW = 768
CV_GFFN = 816
CV_FCW = 848
CV_GFIN = 1364
NCV = 1396


class _Op:
    __slots__ = ("eng", "fn", "deps", "dma", "dma_val", "signal", "sig_val", "pos")


class Sched:
    ENGS = ("pe", "act", "dve", "pool", "sp")
    WINDOW = 6

    def __init__(self):
        self.ops = []
        self.cnt = {e: 0 for e in self.ENGS}
        self.last_w = {}
        self.readers = {}
        self.dma_cnt = {}

    def op(self, eng, fn, reads=(), writes=(), dma=None):
        o = _Op()
        o.eng = eng
        o.fn = fn
        o.dma = dma
        o.signal = False
        o.sig_val = 0
        o.pos = self.cnt[eng]
        self.cnt[eng] += 1
        deps = {}
        for r in reads:
            w = self.last_w.get(r)
            if w is not None:
                deps[w] = True
        for r in writes:
            w = self.last_w.get(r)
            if w is not None and w not in deps:
                deps[w] = False
            for rd in self.readers.get(r, ()):
                if rd not in deps:
                    deps[rd] = False
        for r in reads:
            self.readers.setdefault(r, []).append(o)
        for r in writes:
            self.last_w[r] = o
            self.readers[r] = []
        deps.pop(o, None)
        o.deps = deps
        if dma is not None:
            self.dma_cnt[dma] = self.dma_cnt.get(dma, 0) + 1
            o.dma_val = 16 * self.dma_cnt[dma]
        else:
            o.dma_val = 0
        self.ops.append(o)
        return o

    def _needs(self, d, o, raw):
        if d.dma is not None:
            return True
        if d.eng != o.eng:
            return True
        if o.dma is not None:
            return True
        if d.eng == "pe":
            return False
        return raw and (o.pos - d.pos) <= self.WINDOW

    def emit(self, nc, eng_sems, dma_sems):
        engh = {"pe": nc.tensor, "act": nc.scalar, "dve": nc.vector, "pool": nc.gpsimd, "sp": nc.sync}
        for o in self.ops:
            for d, raw in o.deps.items():
                if d.dma is None and self._needs(d, o, raw):
                    d.signal = True
        run = {e: 0 for e in self.ENGS}
        for o in self.ops:
            if o.dma is None and o.signal:
                run[o.eng] += 1
                o.sig_val = run[o.eng]
        seen = {e: {} for e in self.ENGS}
        for o in self.ops:
            h = engh[o.eng]
            waits = {}
            for d, raw in o.deps.items():
                if not self._needs(d, o, raw):
                    continue
                if d.dma is not None:
                    key, val = ("dma", d.dma), d.dma_val
                else:
                    key, val = ("eng", d.eng), d.sig_val
                if waits.get(key, 0) < val:
                    waits[key] = val
            sn = seen[o.eng]
            for key, val in waits.items():
                if sn.get(key, 0) >= val:
                    continue
                sn[key] = val
                sem = dma_sems[key[1]] if key[0] == "dma" else eng_sems[key[1]]
                h.wait_ge(sem, val)
            ins = o.fn(h)
            if o.dma is not None:
                ins.then_inc(dma_sems[o.dma], 16)
            elif o.signal:
                ins.then_inc(eng_sems[o.eng], 1)
        return {k: 16 * v for k, v in self.dma_cnt.items()}


def _halves(lo, ov=0):
    n = TW - lo
    h0 = lo + ((n // 2 + 1) // 2) * 2
    return [(lo, h0), (h0 - ov, TW)]


def build_program():
    nc = bass.Bass("TRN2", target_bir_lowering=False)
    S = Sched()

    def din(name, shape):
        return nc.dram_tensor(name, list(shape), F32, kind="ExternalInput")

    xin = din("xin", [NT, 128, NSL, TW])
    cbc = din("cbc", [128, D])
    hmk = din("hmk", [128, NT])
    cvec = din("cvec", [L, 128, NCV])
    w_in = din("w_in", [L, 144, 128, 32, 128])
    w_a = din("w_a", [L, 32, 128, 16, 128])
    w_b = din("w_b", [L, 32, 128, 16, 128])
    w_o = din("w_o", [L, 32, 128, 32, 128])
    w_up = din("w_up", [L, 172, 128, 32, 128])
    w_dn = din("w_dn", [L, 4, 32, 128, 22, 128])
    adaT = din("adaT", [L, 192, 128, D])
    outT = nc.dram_tensor("outT", [NT, 128, NSL, OWN], F32, kind="ExternalOutput")
    xsv = nc.dram_tensor("xsv", [NT, 128, NSL, TW], F32, kind="Internal")

    es = ExitStack()
    with es:
        es.enter_context(nc.allow_low_precision("bf16 matmul operands by design"))
        bigf = [es.enter_context(nc.sbuf_tensor(f"big{r}", [128, 16 * TW], F32)) for r in range(4)]
        bigb = [t.bitcast(BF16) for t in bigf]
        wsl = [es.enter_context(nc.sbuf_tensor(f"wsl{i}", [128, 32, 128], BF16)) for i in range(NSLOT)]
        cv = es.enter_context(nc.sbuf_tensor("cv", [128, L * NCV], F32))
        cact = es.enter_context(nc.sbuf_tensor("cact", [128, D], BF16))
        modr = es.enter_context(nc.sbuf_tensor("modr", [128, L * 192], F32))
        modc = es.enter_context(nc.sbuf_tensor("modc", [128, L * 64], F32))
        hm = es.enter_context(nc.sbuf_tensor("hm", [128, NT], F32))
        ones = es.enter_context(nc.sbuf_tensor("ones", [128, 128], F32))
        rsb = es.enter_context(nc.sbuf_tensor("rsb", [128, 3 * TW], F32))
        dummy = es.enter_context(nc.sbuf_tensor("dmy", [128, 8], F32))
        psb = [es.enter_context(nc.psum_tensor(f"ps{i}", [128, 512], F32)) for i in range(8)]
        eng_sems = {e: es.enter_context(nc.semaphore(f"s_{e}")) for e in Sched.ENGS}
        dma_keys = ([f"w{i}" for i in range(NSLOT)] + ["xl0", "xl1", "ev0", "ev1", "xt0", "xt1",
                                                        "out0", "out1", "cst0", "cst1", "cst2", "cst3"])
        dma_sems = {k: es.enter_context(nc.semaphore(f"d_{k}")) for k in dma_keys}

        def RF(r, i):
            return [("b", r, 2 * i), ("b", r, 2 * i + 1)]

        def RB(r, u):
            return [("b", r, u)]

        def PB(p):
            return [("ps", 2 * p), ("ps", 2 * p + 1)]

        def fsl(r, i, c0=0, c1=TW):
            return bigf[r][:, i * TW + c0: i * TW + c1]

        def bsl(r, u, c0=0, c1=TW):
            return bigb[r][:, u * TW + c0: u * TW + c1]

        def cvc(l, col, n=1):
            return cv[:, l * NCV + col: l * NCV + col + n]

        def A(out, in_, func, reads, writes, bias=None, scale=None):
            kw = {}
            if bias is not None:
                kw["bias"] = bias
            if scale is not None:
                kw["scale"] = scale
            S.op("act", lambda e: e.activation(out=out, in_=in_, func=func, **kw), reads=reads, writes=writes)

        def TT(out, in0, in1, op, reads, writes):
            S.op("dve", lambda e: e.tensor_tensor(out=out, in0=in0, in1=in1, op=op), reads=reads, writes=writes)

        def TS(out, in0, s1, s2, op0, op1, reads, writes):
            if op1 is None:
                S.op("dve", lambda e: e.tensor_scalar(out=out, in0=in0, scalar1=s1, scalar2=None, op0=op0),
                     reads=reads, writes=writes)
            else:
                S.op("dve", lambda e: e.tensor_scalar(out=out, in0=in0, scalar1=s1, scalar2=s2, op0=op0, op1=op1),
                     reads=reads, writes=writes)

        def STT(out, in0, scalar, in1, op0, op1, reads, writes, accum_out=None):
            if accum_out is None:
                S.op("dve", lambda e: e.scalar_tensor_tensor(out=out, in0=in0, scalar=scalar, in1=in1, op0=op0,
                                                             op1=op1), reads=reads, writes=writes)
            else:
                S.op("dve", lambda e: e.scalar_tensor_tensor(out=out, in0=in0, scalar=scalar, in1=in1, op0=op0,
                                                             op1=op1, accum_out=accum_out),
                     reads=reads, writes=writes)

        def RECIP(out, in_, reads, writes):
            S.op("dve", lambda e: e.reciprocal(out=out, in_=in_), reads=reads, writes=writes)

        def MEMSET(out, val, reads, writes):
            S.op("dve", lambda e: e.memset(out, val), reads=reads, writes=writes)

        def DMA(eng, out, in_, reads, writes, key):
            S.op(eng, lambda e: e.dma_start(out=out, in_=in_), reads=reads, writes=writes, dma=key)

        wstate = {"n": 0, "eq": 0.0, "credit": 0.0}
        bg = []

        def wtile(src_ap, kc):
            s = wstate["n"] % NSLOT
            wstate["n"] += 1
            DMA("pool", wsl[s][:, 0:kc, :], src_ap, [], [("w", s)], f"w{s}")
            wstate["eq"] += kc / 32.0
            rate = 0.25 if wstate["eq"] < 80 else (0.72 if wstate["eq"] < 245 else 0.30)
            wstate["credit"] += rate * kc / 32.0
            while wstate["credit"] >= 1.0 and bg:
                wstate["credit"] -= 1.0
                bg.pop(0)()
            return s

        def ada_tile(l, j):
            s = wstate["n"] % NSLOT
            wstate["n"] += 1
            dst = wsl[s][:, :, :].rearrange("p a b -> p (a b)")
            DMA("pool", dst, adaT[l, j], [], [("w", s)], f"w{s}")
            acc = modr[:, l * 192 + j: l * 192 + j + 1]
            STT(dst, dst, 1.0, cact[:, :], ALU.mult, ALU.mult, [("w", s), ("cact",)], [("w", s), ("modr", l, j)],
                accum_out=acc)

        pstate = {"n": 0, "set": (0, 1, 2, 3)}

        def ppair():
            st = pstate["set"]
            p = st[pstate["n"] % len(st)]
            pstate["n"] += 1
            return p

        def mmgroup(slot, kc_n, rhs_fn, hv, p, reads, kc_reads=None):
            plan = []
            for kc in range(kc_n):
                for i, (c0, c1) in enumerate(hv):
                    plan.append((psb[2 * p + i][:, 0:c1 - c0], wsl[slot][:, kc, :], rhs_fn(kc, c0, c1),
                                 kc == 0, kc == kc_n - 1))

            def mk(sub):
                def fn(pe):
                    last = None
                    for (o, lh, rh, st, sp_) in sub:
                        last = pe.matmul(o, lhsT=lh, rhs=rh, start=st, stop=sp_)
                    return last
                return fn
            if kc_reads is None:
                S.op("pe", mk(plan), reads=[("w", slot)] + list(reads), writes=PB(p))
            else:
                nh = len(hv)
                for kc in range(kc_n):
                    S.op("pe", mk(plan[kc * nh:(kc + 1) * nh]), reads=[("w", slot)] + list(kc_reads(kc)),
                         writes=PB(p))

        def statmm(src_fn, hv, p, first, last, reads):
            plan = [(psb[2 * p + i][:, 0:c1 - c0], src_fn(c0, c1)) for i, (c0, c1) in enumerate(hv)]

            def fn(pe):
                ins = None
                for (o, rh) in plan:
                    ins = pe.matmul(o, lhsT=ones[:, :], rhs=rh, start=first, stop=last)
                return ins
            S.op("pe", fn, reads=list(reads), writes=PB(p))

        DMA("sp", cv[:, :].rearrange("p (l n) -> p l n", l=L), cvec.ap().rearrange("l p n -> p l n"), [],
            [("cv",)], "cst0")
        DMA("sp", hm[:, :], hmk.ap(), [], [("hm",)], "cst1")
        ctmp = bigf[3][:, 0:D]
        cres = [("b", 3, u) for u in range(16)]
        DMA("sp", ctmp, cbc.ap(), [], cres, "cst2")
        A(cact[:, :], ctmp, AF.Silu, cres, [("cact",)])
        MEMSET(ones[:, :], 1.0, [], [("ones",)])

        def mod_finish(l, js, tag):
            j0, j1 = js
            mres = [("modr", l, j) for j in range(j0, j1)]
            MEMSET(dummy[:, :], 0.0, mres, [("dummy",)])
            sl = modr[:, l * 192 + j0: l * 192 + j1]
            TT(sl, sl, cvc(l, CV_BADA + j0, j1 - j0), ALU.add, [("dummy",), ("cv",)] + mres, [("modf", l, tag)])

        def mod_derive(l, which):
            sc0 = 32 if which == 1 else 128
            gcol = CV_GMIX if which == 1 else CV_GFFN
            dst = modc[:, l * 64 + (which - 1) * 32: l * 64 + which * 32]
            STT(dst, modr[:, l * 192 + sc0: l * 192 + sc0 + 32], 1.0, cvc(l, gcol, 32), ALU.add, ALU.mult,
                [("modf", l, f"s{which}"), ("cv",)], [("modA", l, which)])

        def modcol(l, grp, s):
            return modr[:, l * 192 + grp * 32 + s: l * 192 + grp * 32 + s + 1]

        def Acol(l, which, s):
            return modc[:, l * 64 + (which - 1) * 32 + s: l * 64 + (which - 1) * 32 + s + 1]

        def bg_step(n=1):
            pass

        for j in range(0, 64):
            ada_tile(0, j)
        mod_finish(0, (0, 64), "s1")
        mod_derive(0, 1)

        def _mk(l, j):
            return lambda: ada_tile(l, j)

        def _fin(l, js, tag, which):
            def f():
                mod_finish(l, js, tag)
                if which:
                    mod_derive(l, which)
            return f
        for j in range(64, 96):
            bg.append(_mk(0, j))
        bg.append(_fin(0, (64, 96), "g1", 0))
        for j in range(96, 160):
            bg.append(_mk(0, j))
        bg.append(_fin(0, (96, 160), "s2", 2))
        for j in range(160, 192):
            bg.append(_mk(0, j))
        bg.append(_fin(0, (160, 192), "g2", 0))
        for ll in range(1, L):
            for j in range(0, 64):
                bg.append(_mk(ll, j))
            bg.append(_fin(ll, (0, 64), "s1", 1))
            for j in range(64, 96):
                bg.append(_mk(ll, j))
            bg.append(_fin(ll, (64, 96), "g1", 0))
            for j in range(96, 160):
                bg.append(_mk(ll, j))
            bg.append(_fin(ll, (96, 160), "s2", 2))
            for j in range(160, 192):
                bg.append(_mk(ll, j))
            bg.append(_fin(ll, (160, 192), "g2", 0))

        def bg_flush_until(l, tag):
            while bg and ("modf", l, tag) not in S.last_w:
                bg.pop(0)()

        def x_loc(xr, s):
            return xr[s // 16], s % 16

        def rs(slot, c0, c1):
            return rsb[:, slot * TW + c0: slot * TW + c1]

        def rstd_from_stats(p, hv, ndim, eps, slot):
            for i, (c0, c1) in enumerate(hv):
                A(rs(slot, c0, c1), psb[2 * p + i][:, 0:c1 - c0], AF.Sqrt, [("ps", 2 * p + i)], [("rsb", slot, i)],
                  bias=eps, scale=1.0 / ndim)
                RECIP(rs(slot, c0, c1), rs(slot, c0, c1), [("rsb", slot, i)], [("rsb", slot, i)])

        def make_h(l, which, xr, hreg, treg, lo, slot):
            bgrp = 0 if which == 1 else 3
            for s in range(NSL):
                r, i = x_loc(xr, s)
                tt = s % 2
                STT(fsl(treg, tt, lo), fsl(r, i, lo), Acol(l, which, s), rs(slot, lo, TW), ALU.mult, ALU.mult,
                    RF(r, i) + [("rsb", slot, 0), ("rsb", slot, 1), ("modA", l, which)], RF(treg, tt))
                A(bsl(hreg, s, lo), fsl(treg, tt, lo), AF.Identity, RF(treg, tt) + [("modf", l, f"s{which}")],
                  RB(hreg, s), bias=modcol(l, bgrp, s), scale=1.0)

        def x_stats_acc(r, i, treg, sqslab, accslab, first, lo):
            if first:
                A(fsl(treg, accslab, lo), fsl(r, i, lo), AF.Square, RF(r, i), RF(treg, accslab))
            else:
                A(fsl(treg, sqslab, lo), fsl(r, i, lo), AF.Square, RF(r, i), RF(treg, sqslab))
                TT(fsl(treg, accslab, lo), fsl(treg, accslab, lo), fsl(treg, sqslab, lo), ALU.add,
                   RF(treg, accslab) + RF(treg, sqslab), RF(treg, accslab))

        def x_stats_fin(treg, accslab, hv):
            p = ppair()
            statmm(lambda c0, c1: fsl(treg, accslab, c0, c1), hv, p, True, True, RF(treg, accslab) + [("ones",)])
            return p

        def quarter(t, l, xr, free, first, nxt):
            hmt = hm[:, t:t + 1]
            src = xin if l == 0 else xsv
            pstate["set"] = (0, 1, 2, 3)
            if l == 0:
                hv0 = _halves(0)
                for s in range(NSL):
                    r, i = x_loc(xr, s)
                    x_stats_acc(r, i, free[1], 12 + (s % 2), 14, s == 0, 0)
                rstd_from_stats(x_stats_fin(free[1], 14, hv0), hv0, float(D), RMS_EPS, t)

            if True:
                zl = 0 if l == 0 else 32
                cl = zl + 30
                gl = cl + 2
                ra, rb = xr
                rc, rd = free
                hreg, treg = rc, rd
                bg_flush_until(l, "s1")
                make_h(l, 1, xr, hreg, treg, zl, t)
                pstate["set"] = (0, 1, 2, 3)
                hvz = _halves(zl)
                hvc = _halves(cl)
                hreads = [("b", hreg, u) for u in range(32)]

                def hrhs(kc, c0, c1, hreg=hreg):
                    return bsl(hreg, kc, c0, c1)

                def zgroup(tile_idx, hv, first=False):
                    slot = wtile(w_in[l, tile_idx], 32)
                    p = ppair()
                    mmgroup(slot, 32, hrhs, hv, p, hreads,
                            kc_reads=((lambda kc, hreg=hreg: [("b", hreg, kc)]) if first else None))
                    return p

                for i in range(16):
                    pv = zgroup(i, hvz, first=(i == 0))
                    pg = zgroup(16 + i, hvz)
                    sg = 2 + (i % 2)
                    ut = 4 + (i % 2)
                    for k, (c0, c1) in enumerate(hvz):
                        A(fsl(treg, sg, c0, c1), psb[2 * pg + k][:, 0:c1 - c0], AF.Sigmoid, [("ps", 2 * pg + k)],
                          RF(treg, sg))
                        TT(fsl(treg, ut, c0, c1), psb[2 * pv + k][:, 0:c1 - c0], fsl(treg, sg, c0, c1), ALU.mult,
                           [("ps", 2 * pv + k)] + RF(treg, sg), RF(treg, ut))
                    if zl < HALO:
                        TS(fsl(treg, ut, zl, HALO), fsl(treg, ut, zl, HALO), hmt, None, ALU.mult, None,
                           RF(treg, ut) + [("hm",)], RF(treg, ut))
                    TS(fsl(ra, i, cl), fsl(treg, ut, cl, TW), cvc(l, CV_CONFW + 30 * 16 + i), cvc(l, CV_CONFB + i),
                       ALU.mult, ALU.add, RF(treg, ut) + [("cv",)], RF(ra, i))
                    for k in range(30):
                        sh = 30 - k
                        STT(fsl(ra, i, cl), fsl(treg, ut, cl - sh, TW - sh), cvc(l, CV_CONFW + k * 16 + i),
                            fsl(ra, i, cl), ALU.mult, ALU.add, RF(treg, ut) + RF(ra, i) + [("cv",)], RF(ra, i))
                    if i == 0:
                        A(fsl(treg, 0, cl), fsl(ra, i, cl), AF.Identity, RF(ra, i), RF(treg, 0))
                        A(fsl(treg, 1, cl), fsl(ra, i, cl), AF.Square, RF(ra, i), RF(treg, 1))
                    else:
                        sq = 12 + (i % 2)
                        TT(fsl(treg, 0, cl), fsl(treg, 0, cl), fsl(ra, i, cl), ALU.add, RF(treg, 0) + RF(ra, i),
                           RF(treg, 0))
                        A(fsl(treg, sq, cl), fsl(ra, i, cl), AF.Square, RF(ra, i), RF(treg, sq))
                        TT(fsl(treg, 1, cl), fsl(treg, 1, cl), fsl(treg, sq, cl), ALU.add, RF(treg, 1) + RF(treg, sq),
                           RF(treg, 1))
                    pc = zgroup(48 + i, hvz)
                    px = zgroup(64 + i, hvz)
                    pb = zgroup(32 + i, hvz)
                    sc_t = 6 + (i % 2)
                    cx_t = 8 + (i % 2)
                    v_t = 10 + (i % 2)
                    for k, (c0, c1) in enumerate(hvz):
                        A(fsl(treg, sc_t, c0, c1), psb[2 * pc + k][:, 0:c1 - c0], AF.Identity, [("ps", 2 * pc + k)],
                          RF(treg, sc_t))
                        TT(fsl(treg, cx_t, c0, c1), psb[2 * px + k][:, 0:c1 - c0], fsl(treg, sc_t, c0, c1), ALU.mult,
                           [("ps", 2 * px + k)] + RF(treg, sc_t), RF(treg, cx_t))
                    if zl < HALO:
                        TS(fsl(treg, cx_t, zl, HALO), fsl(treg, cx_t, zl, HALO), hmt, None, ALU.mult, None,
                           RF(treg, cx_t) + [("hm",)], RF(treg, cx_t))
                    TS(fsl(treg, v_t, cl), fsl(treg, cx_t, cl, TW), cvc(l, CV_SCW + 2 * 16 + i), None, ALU.mult, None,
                       RF(treg, cx_t) + [("cv",)], RF(treg, v_t))
                    for k in range(2):
                        sh = 2 - k
                        STT(fsl(treg, v_t, cl), fsl(treg, cx_t, cl - sh, TW - sh), cvc(l, CV_SCW + k * 16 + i),
                            fsl(treg, v_t, cl), ALU.mult, ALU.add, RF(treg, cx_t) + RF(treg, v_t) + [("cv",)],
                            RF(treg, v_t))
                    for k, (c0, c1) in enumerate(hvz):
                        a0 = max(c0, cl)
                        TT(bsl(rb, 16 + i, a0, c1), psb[2 * pb + k][:, a0 - c0:c1 - c0], fsl(treg, v_t, a0, c1),
                           ALU.mult, [("ps", 2 * pb + k)] + RF(treg, v_t), RB(rb, 16 + i))

                ureads = [("b", rb, u) for u in range(16)]
                sreads = [("b", rb, u) for u in range(16, 32)]

                def urhs(kc, c0, c1, rb=rb):
                    return bsl(rb, kc, c0, c1)

                def srhs(kc, c0, c1, rb=rb):
                    return bsl(rb, 16 + kc, c0, c1)

                def gates(f):
                    sga = 2 + (f % 2)
                    sgb = 4 + (f % 2)
                    pga = zgroup(80 + f, hvc)
                    for k, (c0, c1) in enumerate(hvc):
                        A(fsl(treg, sga, c0, c1), psb[2 * pga + k][:, 0:c1 - c0], AF.Sigmoid, [("ps", 2 * pga + k)],
                          RF(treg, sga))
                    pgb = zgroup(112 + f, hvc)
                    for k, (c0, c1) in enumerate(hvc):
                        A(fsl(treg, sgb, c0, c1), psb[2 * pgb + k][:, 0:c1 - c0], AF.Sigmoid, [("ps", 2 * pgb + k)],
                          RF(treg, sgb))

                def ys(f):
                    sga = 2 + (f % 2)
                    sgb = 4 + (f % 2)
                    slot = wtile(w_a[l, f], 16)
                    pya = ppair()
                    mmgroup(slot, 16, urhs, hvc, pya, ureads)
                    for k, (c0, c1) in enumerate(hvc):
                        TT(fsl(treg, sga, c0, c1), psb[2 * pya + k][:, 0:c1 - c0], fsl(treg, sga, c0, c1), ALU.mult,
                           [("ps", 2 * pya + k)] + RF(treg, sga), RF(treg, sga))
                    slot = wtile(w_b[l, f], 16)
                    pyb = ppair()
                    mmgroup(slot, 16, srhs, hvc, pyb, sreads)
                    for k, (c0, c1) in enumerate(hvc):
                        TT(fsl(treg, sgb, c0, c1), psb[2 * pyb + k][:, 0:c1 - c0], fsl(treg, sgb, c0, c1), ALU.mult,
                           [("ps", 2 * pyb + k)] + RF(treg, sgb), RF(treg, sgb))
                    TT(bsl(ra, f, cl), fsl(treg, sga, cl), fsl(treg, sgb, cl), ALU.add,
                       RF(treg, sga) + RF(treg, sgb), RB(ra, f))

                gates(0)
                p1 = ppair()
                statmm(lambda c0, c1: fsl(treg, 0, c0, c1), hvc, p1, True, True, RF(treg, 0) + [("ones",)])
                p2 = ppair()
                statmm(lambda c0, c1: fsl(treg, 1, c0, c1), hvc, p2, True, True, RF(treg, 1) + [("ones",)])
                MU, RI, T2 = 14, 15, 13
                for k, (c0, c1) in enumerate(hvc):
                    A(fsl(treg, MU, c0, c1), psb[2 * p1 + k][:, 0:c1 - c0], AF.Identity, [("ps", 2 * p1 + k)],
                      RF(treg, MU), scale=1.0 / DC)
                TT(fsl(treg, T2, cl), fsl(treg, MU, cl), fsl(treg, MU, cl), ALU.mult, RF(treg, MU), RF(treg, T2))
                for k, (c0, c1) in enumerate(hvc):
                    STT(fsl(treg, RI, c0, c1), psb[2 * p2 + k][:, 0:c1 - c0], 1.0 / DC, fsl(treg, T2, c0, c1),
                        ALU.mult, ALU.subtract, [("ps", 2 * p2 + k)] + RF(treg, T2), RF(treg, RI))
                A(fsl(treg, RI, cl), fsl(treg, RI, cl), AF.Sqrt, RF(treg, RI), RF(treg, RI), bias=LN_EPS, scale=1.0)
                RECIP(fsl(treg, RI, cl), fsl(treg, RI, cl), RF(treg, RI), RF(treg, RI))
                for i in range(16):
                    TT(fsl(ra, i, cl), fsl(ra, i, cl), fsl(treg, MU, cl), ALU.subtract, RF(ra, i) + RF(treg, MU),
                       RF(ra, i))
                    TT(fsl(ra, i, cl), fsl(ra, i, cl), fsl(treg, RI, cl), ALU.mult, RF(ra, i) + RF(treg, RI),
                       RF(ra, i))
                    A(bsl(rb, i, cl), fsl(ra, i, cl), AF.Silu, RF(ra, i) + [("cv",)], RB(rb, i),
                      bias=cvc(l, CV_LNB + i), scale=cvc(l, CV_LNG + i))
                gates(1)
                for f in range(32):
                    ys(f)
                    if f + 2 < 32:
                        gates(f + 2)

                bg_flush_until(l, "g1")
                nxr = (rc, rb)
                mreads = [("b", ra, u) for u in range(32)]

                def mrhs(kc, c0, c1, ra=ra):
                    return bsl(ra, kc, c0, c1)

                for f in range(32):
                    slot = wtile(w_o[l, f], 32)
                    p = ppair()
                    mmgroup(slot, 32, mrhs, hvc, p, mreads)
                    bg_step()
                    xt = 6 + (f % 2)
                    DMA("sp", fsl(treg, xt), src[t, :, f, :], ([("xsv", t, f // 16)] if l > 0 else []),
                        RF(treg, xt), f"xt{f % 2}")
                    nr, ni = x_loc(nxr, f)
                    for k, (c0, c1) in enumerate(hvc):
                        STT(fsl(nr, ni, c0, c1), psb[2 * p + k][:, 0:c1 - c0], modcol(l, 2, f), fsl(treg, xt, c0, c1),
                            ALU.mult, ALU.add, [("ps", 2 * p + k), ("modf", l, "g1")] + RF(treg, xt), RF(nr, ni))
                    x_stats_acc(nr, ni, treg, 12 + (f % 2), 14, f == 0, cl)
                rstd_from_stats(x_stats_fin(treg, 14, hvc), hvc, float(D), RMS_EPS, 2)
                xr = nxr
                free = [ra, rd]

                hreg, treg = ra, rd
                bg_flush_until(l, "s2")
                make_h(l, 2, xr, hreg, treg, cl, 2)
                hreads = [("b", hreg, u) for u in range(32)]

                def hrhs2(kc, c0, c1, hreg=hreg):
                    return bsl(hreg, kc, c0, c1)

                hvu = _halves(cl, ov=2)
                hvg = _halves(gl)
                (a0_, a1_), (b0_, b1_) = hvu
                for q in range(4):
                    for cc in range(QSZ[q]):
                        c = QOFF[q] + cc
                        outs = []
                        for half, fcol in ((0, c), (1, NGC + c)):
                            slot = wtile(w_up[l, fcol], 32)
                            p = ppair()
                            mmgroup(slot, 32, hrhs2, hvu, p, hreads,
                                    kc_reads=((lambda kc, hreg=hreg: [("b", hreg, kc)])
                                              if (q == 0 and cc == 0 and half == 0) else None))
                            bg_step()
                            ct = (11 if half == 0 else 13) + (cc % 2)
                            outs.append(ct)
                            if cl < HALO:
                                TS(psb[2 * p][:, 0:HALO - cl], psb[2 * p][:, 0:HALO - cl], hmt, None, ALU.mult, None,
                                   [("ps", 2 * p), ("hm",)], [("ps", 2 * p)])
                            for k, (o0, o1, base) in enumerate(((gl, a1_, a0_), (a1_, TW, b0_))):
                                A(fsl(treg, ct, o0, o1), psb[2 * p + k][:, o0 - base:o1 - base], AF.Identity,
                                  [("ps", 2 * p + k), ("cv",)], RF(treg, ct), scale=cvc(l, CV_FCW + 2 * 172 + fcol))
                                for tap in range(2):
                                    sh = 2 - tap
                                    STT(fsl(treg, ct, o0, o1), psb[2 * p + k][:, o0 - sh - base:o1 - sh - base],
                                        cvc(l, CV_FCW + tap * 172 + fcol), fsl(treg, ct, o0, o1), ALU.mult, ALU.add,
                                        [("ps", 2 * p + k), ("cv",)] + RF(treg, ct), RF(treg, ct))
                        gt_, vt_ = outs
                        A(fsl(treg, gt_, gl), fsl(treg, gt_, gl), AF.Silu, RF(treg, gt_), RF(treg, gt_))
                        TT(bsl(treg, cc, gl), fsl(treg, gt_, gl), fsl(treg, vt_, gl), ALU.mult,
                           RF(treg, gt_) + RF(treg, vt_), RB(treg, cc))
                    if q == 3 and nxt is not None:
                        nt_, nl_ = nxt
                        nsrc = xin if nl_ == 0 else xsv
                        DMA("sp", bigf[hreg][:, :].rearrange("p (s j) -> p s j", s=16), nsrc[nt_, :, 0:16, :],
                            ([("xsv", nt_, 0)] if nl_ > 0 else []), [("b", hreg, u) for u in range(32)], "xl0")
                    bg_flush_until(l, "g2")
                    greads = [("b", treg, u) for u in range(QSZ[q])]

                    def grhs(kc, c0, c1, treg=treg):
                        return bsl(treg, kc, c0, c1)

                    for f in range(32):
                        slot = wtile(w_dn[l, q, f][:, 0:QSZ[q], :], QSZ[q])
                        p = ppair()
                        mmgroup(slot, QSZ[q], grhs, hvg, p, greads)
                        bg_step()
                        r, i = x_loc(xr, f)
                        for k, (c0, c1) in enumerate(hvg):
                            STT(fsl(r, i, c0, c1), psb[2 * p + k][:, 0:c1 - c0], modcol(l, 5, f), fsl(r, i, c0, c1),
                                ALU.mult, ALU.add, [("ps", 2 * p + k), ("modf", l, "g2")] + RF(r, i), RF(r, i))
                        if q == 3:
                            x_stats_acc(r, i, treg, 11 + (f % 2), 13, f == 0, gl)
                    if q == 3:
                        rstd_from_stats(x_stats_fin(treg, 13, hvg), hvg, float(D), RMS_EPS, t)
                free = [ra, rd]

            ra, rd = free
            if l == 0:
                for k, rr in enumerate(xr):
                    DMA("sp", xsv[t, :, 16 * k:16 * (k + 1), :], bigf[rr][:, :].rearrange("p (s j) -> p s j", s=16),
                        [("b", rr, u) for u in range(32)], [("xsv", t, k)], f"ev{k}")
            else:
                for s in range(NSL):
                    r, i = x_loc(xr, s)
                    ot = 14 + (s % 2)
                    STT(fsl(rd, ot, HALO), fsl(r, i, HALO), cvc(0, CV_GFIN + s), rs(t, HALO, TW), ALU.mult, ALU.mult,
                        RF(r, i) + [("rsb", t, 0), ("rsb", t, 1), ("cv",)], RF(rd, ot))
                    DMA("sp", outT[t, :, s, :], fsl(rd, ot, HALO), RF(rd, ot), [("out", t, s)], f"out{s % 2}")
            return xr, free

        order = [(0, 0), (1, 0), (0, 1), (1, 1)]
        xr, free = (0, 1), [2, 3]
        for k in range(2):
            DMA("sp", bigf[xr[k]][:, :].rearrange("p (s j) -> p s j", s=16), xin[0, :, 16 * k:16 * (k + 1), :],
                [], [("b", xr[k], u) for u in range(32)], f"xl{k}")
        for qi, (t, l) in enumerate(order):
            nxt = order[qi + 1] if qi + 1 < len(order) else None
            xr, free = quarter(t, l, xr, free, qi == 0, nxt)
            if nxt is not None:
                nt_, nl_ = nxt
                nsrc = xin if nl_ == 0 else xsv
                DMA("sp", bigf[xr[0]][:, :].rearrange("p (s j) -> p s j", s=16), nsrc[nt_, :, 16:32, :],
                    ([("xsv", nt_, 1)] if nl_ > 0 else []), [("b", xr[0], u) for u in range(32)], "xl1")
                xr, free = (free[0], xr[0]), [xr[1], free[1]]
        while bg:
            bg.pop(0)()

        finals = S.emit(nc, eng_sems, dma_sems)
        for k in ("out0", "out1"):
            nc.sync.wait_ge(dma_sems[k], finals[k])
    return nc


def _colmajor(v, n):
    return np.ascontiguousarray(v.reshape(n, 128).T)


def _tile_w(w):
    K, N = w.shape
    return np.ascontiguousarray(w.reshape(K // 128, 128, N // 128, 128).transpose(2, 1, 0, 3))


def _prep_shared(inp):
    f32 = np.float32
    sh = {}
    sh["w_in"] = np.stack([_tile_w(np.asarray(inp["w_in"][l], f32)) for l in range(L)])
    sh["w_a"] = np.stack([_tile_w(np.asarray(inp["w_a_out"][l], f32)) for l in range(L)])
    sh["w_b"] = np.stack([_tile_w(np.asarray(inp["w_b_out"][l], f32)) for l in range(L)])
    sh["w_o"] = np.stack([_tile_w(np.asarray(inp["w_o"][l], f32)) for l in range(L)])
    sh["w_up"] = np.stack([_tile_w(np.asarray(inp["w_up"][l], f32)) for l in range(L)])
    wdn = np.zeros((L, 4, 32, 128, 22, 128), f32)
    for l in range(L):
        wd = _tile_w(np.asarray(inp["w_down"][l], f32))
        for q in range(4):
            wdn[l, q, :, :, 0:QSZ[q], :] = wd[:, :, QOFF[q]:QOFF[q] + QSZ[q], :]
    sh["w_dn"] = wdn
    sh["adaT"] = np.stack([np.ascontiguousarray(np.asarray(inp["w_ada"][l], f32).T).reshape(192, 128, D)
                           for l in range(L)])
    cvs = np.zeros((L, 128, NCV), f32)
    for l in range(L):
        cvs[l, :, CV_BADA:CV_BADA + 192] = _colmajor(np.asarray(inp["b_ada"][l], f32), 192)
        cvs[l, :, CV_GMIX:CV_GMIX + 32] = _colmajor(np.asarray(inp["g_mix"][l], f32), 32)
        cw = np.asarray(inp["conf_w"][l], f32)
        for k in range(31):
            cvs[l, :, CV_CONFW + k * 16:CV_CONFW + (k + 1) * 16] = _colmajor(cw[k], 16)
        cvs[l, :, CV_CONFB:CV_CONFB + 16] = _colmajor(np.asarray(inp["conf_b"][l], f32), 16)
        cvs[l, :, CV_LNG:CV_LNG + 16] = _colmajor(np.asarray(inp["ln_g"][l], f32), 16)
        cvs[l, :, CV_LNB:CV_LNB + 16] = _colmajor(np.asarray(inp["ln_b"][l], f32), 16)
        sw = np.asarray(inp["sconv_w"][l], f32)
        for k in range(3):
            cvs[l, :, CV_SCW + k * 16:CV_SCW + (k + 1) * 16] = _colmajor(sw[k], 16)
        cvs[l, :, CV_GFFN:CV_GFFN + 32] = _colmajor(np.asarray(inp["g_ffn"][l], f32), 32)
        fw = np.asarray(inp["ffn_conv_w"][l], f32)
        for k in range(3):
            cvs[l, :, CV_FCW + k * 172:CV_FCW + (k + 1) * 172] = _colmajor(fw[k], 172)
        cvs[l, :, CV_GFIN:CV_GFIN + 32] = _colmajor(np.asarray(inp["g_final"], f32), 32)
    sh["cvec"] = cvs
    sh["cbc"] = np.ascontiguousarray(np.broadcast_to(np.asarray(inp["c"], f32).reshape(1, D), (128, D)))
    return sh


_NC_CACHE = {}


def kernel(**inputs):
    f32 = np.float32
    x = np.asarray(inputs["x"], f32)[0]
    xpad = np.concatenate([np.zeros((HALO, D), f32), x], axis=0)
    shared = _prep_shared(inputs)
    in_maps = []
    for c in range(NCORES):
        xt = np.empty((NT, 128, NSL, TW), f32)
        for t in range(NT):
            r0 = c * (NT * OWN) + t * OWN
            blk = xpad[r0:r0 + TW]
            xt[t] = blk.T.reshape(NSL, 128, TW).transpose(1, 0, 2)
        hmk = np.ones((128, NT), f32)
        if c == 0:
            hmk[:, 0] = 0.0
        m = dict(shared)
        m["xin"] = xt
        m["hmk"] = hmk
        in_maps.append(m)
    if "nc" not in _NC_CACHE:
        _NC_CACHE["nc"] = build_program()
    nc = _NC_CACHE["nc"]
    res = run_bass_kernel_spmd(nc, in_maps, core_ids=list(range(NCORES)))
    out = np.empty((SEQ, D), f32)
    for c in range(NCORES):
        o = np.asarray(res.results[c]["outT"])
        for t in range(NT):
            r0 = c * (NT * OWN) + t * OWN
            out[r0:r0 + OWN] = o[t].transpose(2, 1, 0).reshape(OWN, D)
    return out[None]
```
